# Optimizing a Trainium2 kernel written in Bass

```python
import math
import jax, jax.numpy as jnp
from jax import lax
import numpy as np

D_MODEL = 1024
BATCH = 32
SEQ = 2048
DEPTH = 4

N_MIXERS = 2
CONV_WIDTH = 3
N_HEADS = 16
N_KV_HEADS = 4
HEAD_DIM = D_MODEL // N_HEADS
GROUP = N_HEADS // N_KV_HEADS
WINDOW = 128
BLOCK = WINDOW
ROPE_THETA = 10000.0
D_FF = 2816
EPS = 1e-5
QKV_WIDTH = (N_HEADS + 2 * N_KV_HEADS) * HEAD_DIM
N_CONV_LAYERS = (DEPTH + 1) // 2
N_ATTN_LAYERS = DEPTH // 2

kernel_name = "hybrid_shortconv_swa_sink_convffn"


def rms_norm(x, g):
    xf = x.astype(jnp.float32)
    y = xf * lax.rsqrt(jnp.mean(xf * xf, axis=-1, keepdims=True) + EPS)
    return (y * g.astype(jnp.float32)).astype(x.dtype)


def causal_dwconv(x, w):
    c = x.shape[-1]
    return lax.conv_general_dilated(
        x, w.astype(x.dtype)[:, None, :], window_strides=(1,),
        padding=[(CONV_WIDTH - 1, 0)], dimension_numbers=("NWC", "WIO", "NWC"),
        feature_group_count=c)


def short_conv_mixer(h, w_in, w_conv, w_out):
    bcv = h @ w_in
    b_gate, c_gate, v = jnp.split(bcv, 3, axis=-1)
    y = b_gate * causal_dwconv(c_gate * v, w_conv)
    return y @ w_out


def rope(x, cos, sin):
    x1, x2 = jnp.split(x, 2, axis=-1)
    c = cos[None, :, None, :]
    s = sin[None, :, None, :]
    return jnp.concatenate([x1 * c - x2 * s, x2 * c + x1 * s], axis=-1)


def swa_sink_attention(h, w_qkv, b_qkv, sinks, w_o, b_o, cos, sin):
    bsz, seq, _ = h.shape
    nb = seq // BLOCK
    qkv = h @ w_qkv + b_qkv
    q_w, kv_w = N_HEADS * HEAD_DIM, N_KV_HEADS * HEAD_DIM
    q = qkv[..., :q_w].reshape(bsz, seq, N_HEADS, HEAD_DIM)
    k = qkv[..., q_w:q_w + kv_w].reshape(bsz, seq, N_KV_HEADS, HEAD_DIM)
    v = qkv[..., q_w + kv_w:].reshape(bsz, seq, N_KV_HEADS, HEAD_DIM)
    q = rope(q, cos, sin)
    k = rope(k, cos, sin)

    q = q.reshape(bsz, nb, BLOCK, N_KV_HEADS, GROUP, HEAD_DIM)
    pad = ((0, 0), (1, 0), (0, 0), (0, 0), (0, 0))
    kp = jnp.pad(k.reshape(bsz, nb, BLOCK, N_KV_HEADS, HEAD_DIM), pad)
    vp = jnp.pad(v.reshape(bsz, nb, BLOCK, N_KV_HEADS, HEAD_DIM), pad)
    k_band = jnp.concatenate([kp[:, :-1], kp[:, 1:]], axis=2)
    v_band = jnp.concatenate([vp[:, :-1], vp[:, 1:]], axis=2)

    scores = jnp.einsum("bnqkgd,bnskd->bnkgqs", q, k_band).astype(jnp.float32)
    scores = scores * (HEAD_DIM ** -0.5)

    blk = jnp.arange(nb)[:, None, None]
    qi = jnp.arange(BLOCK)[None, :, None]
    kj = jnp.arange(2 * BLOCK)[None, None, :]
    q_pos = blk * BLOCK + qi
    k_pos = (blk - 1) * BLOCK + kj
    valid = (k_pos <= q_pos) & (q_pos - k_pos < WINDOW) & (k_pos >= 0)
    scores = jnp.where(valid[None, :, None, None], scores, jnp.finfo(jnp.float32).min)

    sink = sinks.astype(jnp.float32).reshape(N_KV_HEADS, GROUP)[None, None, :, :, None, None]
    m = jnp.maximum(jnp.max(scores, axis=-1, keepdims=True), sink)
    p = jnp.exp(scores - m)
    denom = jnp.sum(p, axis=-1, keepdims=True) + jnp.exp(sink - m)
    probs = (p / denom).astype(v_band.dtype)

    o = jnp.einsum("bnkgqs,bnskd->bnqkgd", probs, v_band)
    o = o.reshape(bsz, seq, N_HEADS * HEAD_DIM)
    return o @ w_o + b_o


def conv_ffn(h, w_in, w_conv, w_down):
    gu = h @ w_in
    g, u = jnp.split(gu, 2, axis=-1)
    g = causal_dwconv(g, w_conv)
    return (jax.nn.silu(g) * u) @ w_down


def setup_inputs(seed: int = 0) -> dict:
    key = jax.random.key(seed)
    ks = jax.random.split(key, 16)
    f32 = jnp.float32

    def w(k, shape, fan_in):
        return jax.random.normal(k, shape, f32) * (fan_in ** -0.5)

    return {
        "x": jax.random.normal(ks[0], (BATCH, SEQ, D_MODEL), f32),
        "norm_mix": 1.0 + 0.02 * jax.random.normal(ks[1], (DEPTH, D_MODEL), f32),
        "norm_ffn": 1.0 + 0.02 * jax.random.normal(ks[2], (DEPTH, D_MODEL), f32),
        "norm_final": 1.0 + 0.02 * jax.random.normal(ks[3], (D_MODEL,), f32),
        "conv_w_in": w(ks[4], (N_CONV_LAYERS, D_MODEL, 3 * D_MODEL), D_MODEL),
        "conv_w_conv": w(ks[5], (N_CONV_LAYERS, CONV_WIDTH, D_MODEL), CONV_WIDTH),
        "conv_w_out": w(ks[6], (N_CONV_LAYERS, D_MODEL, D_MODEL), D_MODEL),
        "attn_w_qkv": w(ks[7], (N_ATTN_LAYERS, D_MODEL, QKV_WIDTH), D_MODEL),
        "attn_b_qkv": 0.02 * jax.random.normal(ks[8], (N_ATTN_LAYERS, QKV_WIDTH), f32),
        "attn_sinks": 0.5 * jax.random.normal(ks[9], (N_ATTN_LAYERS, N_HEADS), f32),
        "attn_w_o": w(ks[10], (N_ATTN_LAYERS, N_HEADS * HEAD_DIM, D_MODEL), N_HEADS * HEAD_DIM),
        "attn_b_o": 0.02 * jax.random.normal(ks[11], (N_ATTN_LAYERS, D_MODEL), f32),
        "ffn_w_in": w(ks[12], (DEPTH, D_MODEL, 2 * D_FF), D_MODEL),
        "ffn_w_conv": w(ks[13], (DEPTH, CONV_WIDTH, D_FF), CONV_WIDTH),
        "ffn_w_down": w(ks[14], (DEPTH, D_FF, D_MODEL), D_FF),
    }


def reference(x, norm_mix, norm_ffn, norm_final, conv_w_in, conv_w_conv, conv_w_out,
              attn_w_qkv, attn_b_qkv, attn_sinks, attn_w_o, attn_b_o,
              ffn_w_in, ffn_w_conv, ffn_w_down):
    seq = x.shape[1]
    pos = jnp.arange(seq, dtype=jnp.float32)
    inv_freq = 1.0 / (ROPE_THETA ** (jnp.arange(0, HEAD_DIM, 2, dtype=jnp.float32) / HEAD_DIM))
    ang = pos[:, None] * inv_freq[None, :]
    cos = jnp.cos(ang).astype(x.dtype)
    sin = jnp.sin(ang).astype(x.dtype)

    for i in range(DEPTH):
        h = rms_norm(x, norm_mix[i])
        j = i // N_MIXERS
        if i % N_MIXERS == 0:
            mix = short_conv_mixer(h, conv_w_in[j], conv_w_conv[j], conv_w_out[j])
        else:
            mix = swa_sink_attention(h, attn_w_qkv[j], attn_b_qkv[j], attn_sinks[j],
                                     attn_w_o[j], attn_b_o[j], cos, sin)
        x = x + mix
        x = x + conv_ffn(rms_norm(x, norm_ffn[i]), ffn_w_in[i], ffn_w_conv[i], ffn_w_down[i])
    return rms_norm(x, norm_final)
```

```python
import numpy as np
from contextlib import ExitStack
import concourse.bass as bass
import concourse.mybir as mybir
from concourse.bass_utils import run_bass_kernel_spmd

F32 = mybir.dt.float32
BF16 = mybir.dt.bfloat16
AF = mybir.ActivationFunctionType
ALU = mybir.AluOpType
ESZ = {F32: 4, BF16: 2}

D = 1024
KC = 8
DFF = 2816
FC = 22
UT = 1024
NT = 2
NBLK = 8
NHEAD = 16
NKV = 4
EPS = 1e-5
N_CORES = 8
SEQ = 2048
SLOT_E = 5632
NSLOT = 3
SEM_ROLL = 8000

C_IDENT = 0
C_PROT = 128
C_ONES = 256
C_MCUR = 384
C_MPREV = 896
C_COS = 1408
C_SIN = 1408 + 2048
C_TOTAL = 1408 + 4096


class Op:
    __slots__ = ("eng", "fn", "dma", "deps", "needed", "sig", "dcount")

    def __init__(self, eng, fn, dma):
        self.eng = eng
        self.fn = fn
        self.dma = dma
        self.deps = []
        self.needed = False
        self.sig = None
        self.dcount = 0


class Sched:
    ENGS = ("pe", "act", "dve", "pool", "sp")

    def __init__(self):
        self.q = {e: [] for e in self.ENGS}
        self.lastw = {}
        self.readers = {}
        self.dma_cnt = {}

    def add(self, eng, fn, reads=(), writes=(), dma=None):
        op = Op(eng, fn, dma)
        deps = {}
        rt = set()
        for r in reads:
            rt.update(r)
        wt = set()
        for w in writes:
            wt.update(w)
        lastw = self.lastw
        readers = self.readers
        for t in rt:
            o = lastw.get(t)
            if o is not None:
                deps[id(o)] = o
        for t in wt:
            o = lastw.get(t)
            if o is not None and (o.dma or dma or o.eng != eng):
                deps[id(o)] = o
            rd = readers.get(t)
            if rd:
                for o2 in rd.values():
                    if o2.dma or dma or o2.eng != eng:
                        deps[id(o2)] = o2
        for o in deps.values():
            o.needed = True
        op.deps = list(deps.values())
        key = ("dma", dma) if dma else eng
        for t in rt:
            rd = readers.get(t)
            if rd is None:
                readers[t] = {key: op}
            else:
                rd[key] = op
        for t in wt:
            lastw[t] = op
            readers[t] = {}
        if dma:
            c = self.dma_cnt.get(dma, 0) + 1
            self.dma_cnt[dma] = c
            op.dcount = c
            op.needed = True
        self.q[eng].append(op)
        return op

    def emit(self, nc, es, final_dma_sems):
        nsig = {}
        for e in self.ENGS:
            c = 0
            for op in self.q[e]:
                if op.dma:
                    continue
                if op.needed:
                    op.sig = c
                    c += 1
            nsig[e] = c
        sems = {}
        for e in self.ENGS:
            n = (nsig[e] + SEM_ROLL - 1) // SEM_ROLL
            for i in range(n):
                sems[(e, i)] = es.enter_context(nc.semaphore("s_%s_%d" % (e, i)))
        dsems = {}
        for name in self.dma_cnt:
            dsems[name] = es.enter_context(nc.semaphore("d_" + name))

        def run(e_obj, ename):
            seen = {}
            for op in self.q[ename]:
                for o in op.deps:
                    if o.dma:
                        key = ("dma", o.dma)
                        val = 16 * o.dcount
                        if seen.get(key, 0) >= val:
                            continue
                        seen[key] = val
                        e_obj.wait_ge(dsems[o.dma], val)
                    else:
                        key = o.eng
                        if seen.get(key, -1) >= o.sig:
                            continue
                        seen[key] = o.sig
                        e_obj.wait_ge(sems[(o.eng, o.sig // SEM_ROLL)], o.sig % SEM_ROLL + 1)
                ins = op.fn(e_obj)
                if op.dma:
                    ins.then_inc(dsems[op.dma], 16)
                elif op.sig is not None:
                    ins.then_inc(sems[(ename, op.sig // SEM_ROLL)], 1)
            if ename == "sp":
                for name in final_dma_sems:
                    if name in self.dma_cnt:
                        e_obj.wait_ge(dsems[name], 16 * self.dma_cnt[name])

        with nc.Block() as block:
            @block.tensor
            def _(e):
                run(e, "pe")

            @block.scalar
            def _(e):
                run(e, "act")

            @block.vector
            def _(e):
                run(e, "dve")

            @block.gpsimd
            def _(e):
                run(e, "pool")

            @block.sync
            def _(e):
                run(e, "sp")


class Acc:
    __slots__ = ("ap", "toks")

    def __init__(self, ap, toks):
        self.ap = ap
        self.toks = toks


class Buf:
    def __init__(self, nc, es, name, nbytes, gran, psum=False):
        self.name = name
        self.nbytes = nbytes
        self.gran = gran
        if psum:
            h = es.enter_context(nc.psum_tensor(name, [128, nbytes // 4], F32))
            self.h = {F32: h, BF16: h.bitcast(BF16)}
        else:
            h = es.enter_context(nc.sbuf_tensor(name, [128, nbytes // 2], BF16))
            self.h = {BF16: h, F32: h.bitcast(F32)}

    def view(self, dtype, boff, shape):
        return View(self, dtype, boff, tuple(shape))


class View:
    def __init__(self, buf, dtype, boff, shape):
        self.buf = buf
        self.dtype = dtype
        self.es = ESZ[dtype]
        assert boff % self.es == 0
        self.eoff = boff // self.es
        self.shape = shape
        st = []
        s = 1
        for n in reversed(shape):
            st.append(s)
            s *= n
        self.strides = tuple(reversed(st))
        self.size = s
        assert boff + s * self.es <= buf.nbytes, (buf.name, boff, s, self.es, buf.nbytes)
        self.rowlen = buf.nbytes // self.es

    def __call__(self, *idx, p=(0, 128), bcast=None):
        assert len(idx) == len(self.shape)
        off = self.eoff
        dims = []
        for ix, st, n in zip(idx, self.strides, self.shape):
            if isinstance(ix, tuple):
                lo, hi = ix
                assert 0 <= lo < hi <= n, (ix, n)
                off += lo * st
                dims.append([st, hi - lo])
            else:
                assert 0 <= ix < n, (ix, n)
                off += ix * st
        es = self.es
        gran = self.buf.gran
        name = self.buf.name
        toks = set()
        if dims and dims[-1][0] == 1:
            outer = dims[:-1]
            run = dims[-1][1]
        else:
            outer = dims
            run = 1
        offs = [off]
        for st, n in outer:
            offs = [o + i * st for o in offs for i in range(n)]
        for o in offs:
            b0 = (o * es) // gran
            b1 = ((o + run) * es - 1) // gran
            for b in range(b0, b1 + 1):
                toks.add((name, b))
        apdims = [list(d) for d in dims]
        if bcast is not None:
            apdims.insert(bcast[0], [0, bcast[1]])
        p0, p1 = p
        ap = bass.AP(self.buf.h[self.dtype], p0 * self.rowlen + off,
                     [[self.rowlen, p1 - p0]] + apdims)
        return Acc(ap, toks)

    def all(self, p=(0, 128)):
        return self(*[(0, n) for n in self.shape], p=p)


def unit_blocks(L):
    bl = []
    for i in range(L):
        if i % 2 == 0:
            for c in range(8):
                bl.append(("cin", i, c, 3072))
            for b in range(2):
                bl.append(("cout", i, b, 4096))
        else:
            for b in range(2):
                bl.append(("q", i, b, 4096))
            bl.append(("k", i, 0, 4096))
            bl.append(("v", i, 0, 2048))
            for b in range(2):
                bl.append(("wo", i, b, 4096))
        for b in range(11):
            bl.append(("fin", i, b, 4096))
        for b in range(4):
            bl.append(("fdn", i, b, 5632))
    return bl


def _kmaj(w):
    K, N = w.shape
    return w.reshape(K // 128, 128, N).transpose(1, 0, 2)


def host_wstream(L, inp):
    parts = []
    for (kind, i, b, ne) in unit_blocks(L):
        j = i // 2
        if kind == "cin":
            w = inp["conv_w_in"][j]
            a = np.stack([_kmaj(w[:, s * 1024 + b * 128: s * 1024 + (b + 1) * 128]) for s in range(3)], axis=2)
        elif kind == "cout":
            a = _kmaj(inp["conv_w_out"][j][:, b * 512:(b + 1) * 512])
        elif kind == "q":
            a = _kmaj(inp["attn_w_qkv"][j][:, b * 512:(b + 1) * 512])
        elif kind == "k":
            wk = _kmaj(inp["attn_w_qkv"][j][:, 1024:1280]).reshape(128, 8, 4, 1, 64)
            a = np.broadcast_to(wk, (128, 8, 4, 2, 64))
        elif kind == "v":
            a = _kmaj(inp["attn_w_qkv"][j][:, 1280:1536])
        elif kind == "wo":
            a = _kmaj(inp["attn_w_o"][j][:, b * 512:(b + 1) * 512])
        elif kind == "fin":
            w = inp["ffn_w_in"][i]
            a = np.stack([_kmaj(w[:, s * DFF + b * 256: s * DFF + (b + 1) * 256]) for s in range(2)], axis=2)
        elif kind == "fdn":
            a = _kmaj(inp["ffn_w_down"][i][:, b * 256:(b + 1) * 256])
        a = np.ascontiguousarray(a, dtype=np.float32).reshape(128, -1)
        assert a.shape[1] == ne, (kind, a.shape, ne)
        parts.append(a.reshape(-1))
    return np.concatenate(parts)


def vec_layout():
    lay = {}
    o = 0
    for name, n in (("nmix", 32), ("nffn", 32), ("nfin", 8), ("cconv", 48), ("fconv", 264),
                    ("bq", 16), ("bk", 8), ("bo", 16), ("bv", 512), ("sink", 32)):
        lay[name] = o
        o += n
    lay["_total"] = o
    return lay


VL = vec_layout()


def host_vecs(inp):
    def fm(v):
        sh = v.shape
        k = sh[-1] // 128
        v = v.reshape(sh[:-1] + (k, 128))
        return np.moveaxis(v, -1, 0)

    cols = []
    cols.append(fm(inp["norm_mix"]).reshape(128, -1))
    cols.append(fm(inp["norm_ffn"]).reshape(128, -1))
    cols.append(fm(inp["norm_final"]).reshape(128, -1))
    cols.append(fm(inp["conv_w_conv"]).reshape(128, -1))
    cols.append(fm(inp["ffn_w_conv"]).reshape(128, -1))
    bqkv = inp["attn_b_qkv"]
    cols.append(fm(bqkv[:, :1024]).reshape(128, -1))
    bk = bqkv[:, 1024:1280].reshape(2, 4, 1, 64)
    bk = np.broadcast_to(bk, (2, 4, 2, 64)).reshape(2, 4, 128)
    cols.append(np.ascontiguousarray(bk.transpose(2, 0, 1)).reshape(128, -1))
    cols.append(fm(inp["attn_b_o"]).reshape(128, -1))
    bv = bqkv[:, 1280:1536]
    cols.append(np.broadcast_to(bv.reshape(1, 512), (128, 512)))
    cols.append(np.broadcast_to(inp["attn_sinks"].reshape(1, 32), (128, 32)))
    out = np.ascontiguousarray(np.concatenate(cols, axis=1), dtype=np.float32)
    assert out.shape[1] == VL["_total"], out.shape
    return out


def host_consts():
    c = np.zeros((128, C_TOTAL), np.float32)
    c[:, C_IDENT:C_IDENT + 128] = np.eye(128, dtype=np.float32)
    prot = np.zeros((128, 128), np.float32)
    for m in range(128):
        if m % 64 < 32:
            prot[m + 32, m] = -1.0
        else:
            prot[m - 32, m] = 1.0
    c[:, C_PROT:C_PROT + 128] = prot
    c[:, C_ONES:C_ONES + 128] = 1.0 / 1024.0
    key = np.arange(128)[:, None]
    qi = np.arange(128)[None, :]
    NEG = np.float32(-30000.0)
    mcur = np.where(key <= qi, np.float32(0.0), NEG).astype(np.float32)
    mprev = np.where(key > qi, np.float32(0.0), NEG).astype(np.float32)
    c[:, C_MCUR:C_MCUR + 512] = np.concatenate([mcur, mcur, mprev, mprev], axis=1)
    c[:, C_MPREV:C_MPREV + 512] = 0.0
    pos = np.arange(SEQ, dtype=np.float32)
    inv_freq = (1.0 / (np.float32(10000.0) ** (np.arange(0, 64, 2, dtype=np.float32) / np.float32(64)))).astype(np.float32)
    ang = pos[None, :] * inv_freq[:, None]
    fidx = np.arange(128) % 32
    c[:, C_COS:C_COS + SEQ] = np.cos(ang)[fidx]
    c[:, C_SIN:C_SIN + SEQ] = np.sin(ang)[fidx]
    return c


def build(NU, L):
    nc = bass.Bass("TRN2", target_bir_lowering=False)
    S = Sched()
    es = ExitStack()
    blocks = unit_blocks(L)
    boffs = []
    o = 0
    for b in blocks:
        boffs.append(o)
        o += 128 * b[3]
    wtotal = o

    x_d = nc.dram_tensor("x", [NU * UT, D], F32, kind="ExternalInput")
    w_d = nc.dram_tensor("wstream", [wtotal], F32, kind="ExternalInput")
    v_d = nc.dram_tensor("vecs", [128, VL["_total"]], F32, kind="ExternalInput")
    c_d = nc.dram_tensor("consts", [128, C_TOTAL], F32, kind="ExternalInput")
    o_d = nc.dram_tensor("out", [NU * UT, D], F32, kind="ExternalOutput")

    XB = Buf(nc, es, "X", 32768, 2048)
    HB = Buf(nc, es, "H", 16384, 1024)
    BIG = Buf(nc, es, "BIG", 45056, 1024)
    VB = Buf(nc, es, "V", 4736, 520)
    CSB = Buf(nc, es, "CS", 8192, 8192)
    WB = [Buf(nc, es, "W%d" % i, SLOT_E * 2, SLOT_E * 2) for i in range(NSLOT)]
    IOB = [Buf(nc, es, "IO%d" % i, 4096, 4096) for i in range(2)]
    NRM = Buf(nc, es, "NRM", 12288, 1024)
    SCR = Buf(nc, es, "SCR", 24576, 512)
    CNB = Buf(nc, es, "CN", 512 + 768 + 2048, 4096)
    VEC = Buf(nc, es, "VEC", VL["_total"] * 4, 65536)
    HAL = Buf(nc, es, "HAL", 4096, 64)
    PS = [Buf(nc, es, "PS%d" % i, 2048, 2048, psum=True) for i in range(8)]

    X = XB.view(F32, 0, (KC, UT))
    H = HB.view(BF16, 0, (KC, UT))
    A = BIG.view(BF16, 0, (FC, UT))
    Y = BIG.view(BF16, 0, (KC, UT))
    Q = BIG.view(BF16, 0, (KC, UT))
    Kt = BIG.view(BF16, 16384, (NKV, 128 + UT))
    OT = BIG.view(BF16, 13 * 2048, (KC, UT))
    Vv = VB.view(BF16, 0, (NBLK + 1, NKV, 65))
    COS = CSB.view(BF16, 0, (SEQ,))
    SIN = CSB.view(BF16, 4096, (SEQ,))
    SQ = NRM.view(BF16, 0, (KC, 512))
    RSTD = [NRM.view(F32, 8192 + 2048 * i, (512,)) for i in range(2)]
    IDF = CNB.view(F32, 0, (128,))
    IDB = CNB.view(BF16, 512, (128,))
    PROT = CNB.view(BF16, 768, (128,))
    ONES = CNB.view(BF16, 1024, (128,))
    MCUR = CNB.view(BF16, 1280, (512,))
    MPREV = CNB.view(BF16, 2304, (512,))
    MASKB = CNB.view(BF16, 1280, (512,))
    VECS = VEC.view(F32, 0, (VL["_total"],))
    GHALO = HAL.view(F32, 0, (4, FC, 2))
    CVHALO = HAL.view(F32, 704, (2, KC, 2))
    KHALO = HAL.view(BF16, 832, (2, NKV, 128))
    VHALO = HAL.view(BF16, 2880, (2, NKV, 65))
    ESINK = HAL.view(F32, 3920, (2, 16))
    ESINKF = HAL.view(F32, 3920, (32,))
    EPSC = HAL.view(F32, 4048, (1,))

    def vcol(name, idx):
        c = VL[name] + idx
        return VECS((c, c + 1))

    psn = [0]

    def newps():
        b = PS[psn[0] % 8]
        psn[0] += 1
        return b

    def mm(out, lhsT, rhs, start, stop):
        S.add("pe", lambda e: e.matmul(out.ap, lhsT.ap, rhs.ap, start=start, stop=stop),
              reads=[lhsT.toks, rhs.toks], writes=[out.toks])

    def tr(out, in_, ident):
        S.add("pe", lambda e: e.transpose(out.ap, in_.ap, ident.ap),
              reads=[in_.toks, ident.toks], writes=[out.toks])

    def act(out, in_, func, bias=None, scale=1.0):
        rd = [in_.toks]
        if isinstance(scale, Acc):
            rd.append(scale.toks)
            scale = scale.ap
        if bias is not None:
            rd.append(bias.toks)
            S.add("act", lambda e: e.activation(out.ap, in_.ap, func, bias=bias.ap, scale=scale),
                  reads=rd, writes=[out.toks])
        else:
            S.add("act", lambda e: e.activation(out.ap, in_.ap, func, scale=scale),
                  reads=rd, writes=[out.toks])

    def tt(eng, out, in0, in1, op):
        S.add(eng, lambda e: e.tensor_tensor(out.ap, in0.ap, in1.ap, op),
              reads=[in0.toks, in1.toks], writes=[out.toks])

    def ts(eng, out, in0, s1, s2, op0, op1=None):
        rd = [in0.toks]
        a1 = s1
        a2 = s2
        if isinstance(s1, Acc):
            rd.append(s1.toks)
            a1 = s1.ap
        if isinstance(s2, Acc):
            rd.append(s2.toks)
            a2 = s2.ap
        if op1 is None:
            S.add(eng, lambda e: e.tensor_scalar(out.ap, in0.ap, a1, None, op0),
                  reads=rd, writes=[out.toks])
        else:
            S.add(eng, lambda e: e.tensor_scalar(out.ap, in0.ap, a1, a2, op0, op1),
                  reads=rd, writes=[out.toks])

    def stt(eng, out, in0, sc, in1, op0, op1):
        rd = [in0.toks, in1.toks]
        a = sc
        if isinstance(sc, Acc):
            rd.append(sc.toks)
            a = sc.ap
        S.add(eng, lambda e: e.scalar_tensor_tensor(out.ap, in0.ap, a, in1.ap, op0, op1),
              reads=rd, writes=[out.toks])

    def cp(eng, out, in_):
        S.add(eng, lambda e: e.tensor_copy(out.ap, in_.ap), reads=[in_.toks], writes=[out.toks])

    def memset(eng, out, val):
        S.add(eng, lambda e: e.memset(out.ap, val), writes=[out.toks])

    def recip(out, in_):
        S.add("dve", lambda e: e.reciprocal(out.ap, in_.ap), reads=[in_.toks], writes=[out.toks])

    def dma(eng, sem, out_ap, in_ap, reads=(), writes=()):
        S.add(eng, lambda e: e.dma_start(out_ap, in_ap), reads=reads, writes=writes, dma=sem)

    nblk_unit = len(blocks)
    total_blocks = nblk_unit * NU
    wst = {"k": 0, "issued": 0}

    def issue_block(j):
        kind, i, b, ne = blocks[j % nblk_unit]
        off = boffs[j % nblk_unit]
        slot = WB[j % NSLOT]
        dst = slot.view(BF16, 0, (ne,)).all()
        src = bass.AP(w_d, off, [[ne, 128], [1, ne]])
        dma("pool", "w%d" % (j % NSLOT), dst.ap, src, writes=[dst.toks])

    def next_block(kind):
        k = wst["k"]
        wst["k"] += 1
        assert blocks[k % nblk_unit][0] == kind, (blocks[k % nblk_unit], kind)
        while wst["issued"] < min(total_blocks, k + NSLOT):
            issue_block(wst["issued"])
            wst["issued"] += 1
        return WB[k % NSLOT]

    cap = c_d.ap()
    vap = v_d.ap()
    t = IDF.all()
    dma("sp", "cst0", t.ap, cap[:, C_IDENT:C_IDENT + 128], writes=[t.toks])
    t = VECS.all()
    dma("sp", "cst1", t.ap, vap, writes=[t.toks])
    for ci, (view, c0, n) in enumerate(((IDB, C_IDENT, 128), (PROT, C_PROT, 128), (ONES, C_ONES, 128),
                        (MCUR, C_MCUR, 512), (MPREV, C_MPREV, 512), (COS, C_COS, SEQ), (SIN, C_SIN, SEQ))):
        t = view.all()
        dma("pool", "cstp%d" % ci, t.ap, cap[:, c0:c0 + n], writes=[t.toks])
    t = Vv((0, NBLK + 1), (0, NKV), (64, 65))
    memset("dve", t, 1.0)
    memset("dve", EPSC.all(), EPS)
    SNK = VECS((VL["sink"], VL["sink"] + 32))
    act(ESINKF.all(), SNK, AF.Exp)

    def load_unit(u):
        for b in range(NBLK):
            io = IOB[b % 2].view(F32, 0, (D,))
            t = io.all()
            dma("sp", "io%d" % (b % 2), t.ap, x_d.ap()[u * UT + b * 128: u * UT + (b + 1) * 128, :],
                writes=[t.toks])
            for half in range(2):
                ps = newps().view(F32, 0, (4, 128))
                for q4 in range(4):
                    kc = half * 4 + q4
                    tr(ps(q4, (0, 128)), io((kc * 128, kc * 128 + 128)), IDF.all())
                dst = X((half * 4, half * 4 + 4), (b * 128, b * 128 + 128))
                if half == 0:
                    act(dst, ps.all(), AF.Copy)
                else:
                    cp("dve", dst, ps.all())

    def rmsnorm(gname, gidx, dst_view):
        for t in range(NT):
            tsl = (t * 512, t * 512 + 512)
            act(SQ.all(), X((0, KC), tsl), AF.Square)
            ps = newps().view(F32, 0, (512,))
            for kc in range(KC):
                mm(ps.all(), ONES.all(), SQ(kc, (0, 512)), kc == 0, kc == KC - 1)
            r = RSTD[t % 2]
            act(r.all(), ps.all(), AF.Sqrt, bias=EPSC.all())
            recip(r.all(), r.all())
            for kc in range(KC):
                stt("dve", dst_view(kc, tsl), X(kc, tsl), vcol(gname, gidx * 8 + kc), r.all(),
                    ALU.mult, ALU.mult)

    def conv_taps(eng, src, base, wname, widx, stride, u1, u2, u3):
        w0 = vcol(wname, widx)
        w1 = vcol(wname, widx + stride)
        w2 = vcol(wname, widx + 2 * stride)
        S.add("act", lambda e: e.activation(u1.ap, src((base - 2, base + 510)).ap, AF.Identity, scale=w0.ap),
              reads=[src((base - 2, base + 510)).toks, w0.toks], writes=[u1.toks])
        stt(eng, u2, src((base - 1, base + 511)), w1, u1, ALU.mult, ALU.add)
        stt(eng, u3, src((base, base + 512)), w2, u2, ALU.mult, ALU.add)

    def conv_mixer(i, hh):
        j = i // 2
        CV = [SCR.view(F32, 4112 * k, (1026,)) for k in range(2)]
        SBC = [SCR.view(F32, 8224 + 2048 * k, (512,)) for k in range(2)]
        U1 = [SCR.view(F32, 12320 + 2048 * k, (512,)) for k in range(2)]
        U2 = SCR.view(F32, 16416, (512,))
        U3 = SCR.view(F32, 18464, (512,))
        its = [(c, t) for c in range(KC) for t in range(NT)]
        st = {}

        def stage_a(k):
            c, t = its[k]
            if t == 0:
                st["wb"] = next_block("cin").view(BF16, 0, (KC, 3, 128))
                cv = CV[c % 2]
                if hh == 0:
                    memset("dve", cv((0, 2)), 0.0)
                else:
                    cp("dve", cv((0, 2)), CVHALO(j, c, (0, 2)))
            wb = st["wb"]
            tsl = (t * 512, t * 512 + 512)
            pss = [PS[k % 3].view(F32, 0, (512,)), PS[3 + k % 2].view(F32, 0, (512,)),
                   PS[5 + k % 3].view(F32, 0, (512,))]
            for s_ in (1, 2, 0):
                for kc in range(KC):
                    mm(pss[s_].all(), wb(kc, s_, (0, 128)), H(kc, tsl), kc == 0, kc == KC - 1)
            act(SBC[k % 2].all(), pss[1].all(), AF.Copy)
            st[k] = pss

        def stage_b(k):
            c, t = its[k]
            cv = CV[c % 2]
            base = 2 + t * 512
            tt("dve", cv((base, base + 512)), SBC[k % 2].all(), st[k][2].all(), ALU.mult)
            act(U1[k % 2].all(), cv((base - 2, base + 510)), AF.Identity, scale=vcol("cconv", j * 24 + c))
            if t == NT - 1:
                cp("dve", CVHALO(j, c, (0, 2)), cv((1024, 1026)))

        def stage_c(k):
            c, t = its[k]
            cv = CV[c % 2]
            base = 2 + t * 512
            tsl = (t * 512, t * 512 + 512)
            w1 = vcol("cconv", j * 24 + c + 8)
            w2 = vcol("cconv", j * 24 + c + 16)
            stt("dve", U2.all(), cv((base - 1, base + 511)), w1, U1[k % 2].all(), ALU.mult, ALU.add)
            stt("dve", U3.all(), cv((base, base + 512)), w2, U2.all(), ALU.mult, ALU.add)
            tt("dve", Y(c, tsl), U3.all(), st.pop(k)[0].all(), ALU.mult)

        nit = len(its)
        for k in range(nit + 2):
            if k < nit:
                stage_a(k)
            if 0 <= k - 1 < nit:
                stage_b(k - 1)
            if 0 <= k - 2 < nit:
                stage_c(k - 2)
        for b in range(2):
            wb = next_block("cout").view(BF16, 0, (KC, 512))
            for t in range(NT):
                tsl = (t * 512, t * 512 + 512)
                for ml in range(4):
                    m = b * 4 + ml
                    ps = newps().view(F32, 0, (512,))
                    for kc in range(KC):
                        mm(ps.all(), wb(kc, (ml * 128, ml * 128 + 128)), Y(kc, tsl), kc == 0, kc == KC - 1)
                    tt("dve", X(m, tsl), X(m, tsl), ps.all(), ALU.add)

    def ffn(i, hh):
        GB = [SCR.view(F32, 4112 * k, (1026,)) for k in range(2)]
        U1 = [SCR.view(F32, 8224 + 2048 * k, (512,)) for k in range(2)]
        U2 = SCR.view(F32, 12320, (512,))
        U3 = [SCR.view(F32, 14368 + 2048 * k, (512,)) for k in range(2)]
        SS = [SCR.view(F32, 18464 + 2048 * k, (512,)) for k in range(2)]
        its = [(b, t, fl) for b in range(11) for t in range(NT) for fl in range(2)]
        st = {}

        def stage_a(k):
            b, t, fl = its[k]
            f = b * 2 + fl
            if t == 0 and fl == 0:
                st["wb"] = next_block("fin").view(BF16, 0, (KC, 2, 256))
                for fl2 in range(2):
                    f2 = b * 2 + fl2
                    if hh == 0:
                        memset("dve", GB[f2 % 2]((0, 2)), 0.0)
                    else:
                        cp("dve", GB[f2 % 2]((0, 2)), GHALO(i, f2, (0, 2)))
            wb = st["wb"]
            gb = GB[f % 2]
            tsl = (t * 512, t * 512 + 512)
            pg = newps().view(F32, 0, (512,))
            pu = newps().view(F32, 0, (512,))
            for s_, ps in ((0, pg), (1, pu)):
                for kc in range(KC):
                    mm(ps.all(), wb(kc, s_, (fl * 128, fl * 128 + 128)), H(kc, tsl), kc == 0, kc == KC - 1)
            base = 2 + t * 512
            act(gb((base, base + 512)), pg.all(), AF.Copy)
            w0 = vcol("fconv", i * 66 + f)
            act(U1[k % 2].all(), gb((base - 2, base + 510)), AF.Identity, scale=w0)
            if t == NT - 1:
                cp("dve", GHALO(i, f, (0, 2)), gb((1024, 1026)))
            st[k] = pu

        def stage_b(k):
            b, t, fl = its[k]
            f = b * 2 + fl
            gb = GB[f % 2]
            base = 2 + t * 512
            w1 = vcol("fconv", i * 66 + f + 22)
            w2 = vcol("fconv", i * 66 + f + 44)
            stt("dve", U2.all(), gb((base - 1, base + 511)), w1, U1[k % 2].all(), ALU.mult, ALU.add)
            stt("dve", U3[k % 2].all(), gb((base, base + 512)), w2, U2.all(), ALU.mult, ALU.add)
            act(SS[k % 2].all(), U3[k % 2].all(), AF.Silu)

        def stage_c(k):
            b, t, fl = its[k]
            f = b * 2 + fl
            tsl = (t * 512, t * 512 + 512)
            tt("dve", A(f, tsl), SS[k % 2].all(), st.pop(k).all(), ALU.mult)

        nit = len(its)
        for k in range(nit + 2):
            if k < nit:
                stage_a(k)
            if 0 <= k - 1 < nit:
                stage_b(k - 1)
            if 0 <= k - 2 < nit:
                stage_c(k - 2)
        for b in range(4):
            wb = next_block("fdn").view(BF16, 0, (FC, 256))
            for t in range(NT):
                tsl = (t * 512, t * 512 + 512)
                for ml in range(2):
                    m = b * 2 + ml
                    ps = newps().view(F32, 0, (512,))
                    for f in range(FC):
                        mm(ps.all(), wb(f, (ml * 128, ml * 128 + 128)), A(f, tsl), f == 0, f == FC - 1)
                    tt("dve", X(m, tsl), X(m, tsl), ps.all(), ALU.add)

    def rope_evac(ps, bias, dst, pos0, k):
        QSB = [SCR.view(BF16, 1024 * n_, (512,)) for n_ in range(2)]
        T1 = [SCR.view(F32, 2048 + 2048 * n_, (512,)) for n_ in range(2)]
        T2 = [SCR.view(F32, 6144 + 2048 * n_, (512,)) for n_ in range(2)]
        qsb = QSB[k % 2]
        t1 = T1[k % 2]
        t2 = T2[k % 2]
        act(qsb.all(), ps.all(), AF.Identity, bias=bias)
        pr = newps().view(F32, 0, (512,))
        mm(pr.all(), PROT.all(), qsb.all(), True, True)
        tt("pool", t1.all(), qsb.all(), COS((pos0, pos0 + 512)), ALU.mult)
        tt("dve", t2.all(), pr.all(), SIN((pos0, pos0 + 512)), ALU.mult)
        tt("dve", dst, t1.all(), t2.all(), ALU.add)

    def attention(i, hh):
        j = i // 2
        PT = [SCR.view(BF16, 10240 + 2048 * n_, (2, 2, 2, 128)) for n_ in range(3)]
        OSB = [SCR.view(BF16, 16384 + 2048 * n_, (NHEAD, 64)) for n_ in range(2)]
        OSBF = [SCR.view(BF16, 16384 + 2048 * n_, (NHEAD * 64,)) for n_ in range(2)]
        DEN = [SCR.view(F32, 20480 + 32 * n_, (4,)) for n_ in range(2)]
        RDEN = [SCR.view(F32, 20544 + 32 * n_, (4,)) for n_ in range(2)]
        rk = 0
        if hh == 1:
            cp("dve", Kt((0, NKV), (0, 128)), KHALO(j, (0, NKV), (0, 128)))
            cp("dve", Vv(0, (0, NKV), (0, 65)), VHALO(j, (0, NKV), (0, 65)))
        for b in range(2):
            wb = next_block("q").view(BF16, 0, (KC, 512))
            for ml in range(4):
                m = b * 4 + ml
                for t in range(NT):
                    tsl = (t * 512, t * 512 + 512)
                    ps = newps().view(F32, 0, (512,))
                    for kc in range(KC):
                        mm(ps.all(), wb(kc, (ml * 128, ml * 128 + 128)), H(kc, tsl), kc == 0, kc == KC - 1)
                    rope_evac(ps, vcol("bq", j * 8 + m), Q(m, tsl), hh * UT + t * 512, rk)
                    rk += 1
        wb = next_block("k").view(BF16, 0, (KC, NKV, 128))
        for g in range(NKV):
            for t in range(NT):
                tsl = (t * 512, t * 512 + 512)
                ps = newps().view(F32, 0, (512,))
                for kc in range(KC):
                    mm(ps.all(), wb(kc, g, (0, 128)), H(kc, tsl), kc == 0, kc == KC - 1)
                rope_evac(ps, vcol("bk", j * 4 + g), Kt(g, (128 + t * 512, 128 + t * 512 + 512)),
                          hh * UT + t * 512, rk)
                rk += 1
        wb = next_block("v").view(BF16, 0, (KC, 256))
        BV = VECS.buf.view(F32, (VL["bv"] + j * 256) * 4, (NKV, 64))
        for bp in range(NBLK // 2):
            psb = newps()
            ps = psb.view(F32, 0, (2, NKV, 64))
            psf = psb.view(F32, 0, (2, 256))
            for bl in range(2):
                blk = bp * 2 + bl
                for kc in range(KC):
                    mm(psf(bl, (0, 256)), H(kc, (blk * 128, blk * 128 + 128)), wb(kc, (0, 256)),
                       kc == 0, kc == KC - 1)
            for bl in range(2):
                blk = bp * 2 + bl
                tt("dve", Vv(1 + blk, (0, NKV), (0, 64)), ps(bl, (0, NKV), (0, 64)), BV.all(), ALU.add)
        def s_stage(n, g, pt):
            nb = hh * NBLK + n
            nkb = 2 if nb > 0 else 1
            bks = [newps() for _ in range(2)]
            banks = [b_.view(F32, 0, (2, 2, 128)) for b_ in bks]
            flat = [b_.view(F32, 0, (512,)) for b_ in bks]
            for hf in range(2):
                mm(flat[hf]((0, nkb * 256)), IDB.all(), MASKB((0, nkb * 256)), True, False)
            for kb in range(nkb):
                kcol = (128 + n * 128) if kb == 0 else n * 128
                for hl in range(4):
                    h = 4 * g + hl
                    ch = h // 2
                    hf = h % 2
                    prt = (hf * 64, hf * 64 + 64)
                    mm(banks[hf](kb, hl // 2, (0, 128)), Kt(g, (kcol, kcol + 128), p=prt),
                       Q(ch, (n * 128, n * 128 + 128), p=prt), False, (kb == nkb - 1 and hl >= 2))
            for hf in range(2):
                sl = ((0, nkb), (0, 2), (0, 128))
                act(pt(hf, *sl), banks[hf](*sl), AF.Exp, scale=0.125)

        def p_stage(n, g, pt):
            nb = hh * NBLK + n
            nkb = 2 if nb > 0 else 1
            osb = OSB[n % 2]
            po = newps().view(F32, 0, (4, 65))
            for hl in range(4):
                for kb in range(nkb):
                    vblk = (1 + n) if kb == 0 else n
                    mm(po(hl, (0, 65)), pt(hl % 2, kb, hl // 2, (0, 128)), Vv(vblk, g, (0, 65)),
                       kb == 0, kb == nkb - 1)
            den = DEN[g % 2]
            rden = RDEN[g % 2]
            tt("dve", den.all(), po((0, 4), 64), ESINK(j, (4 * g, 4 * g + 4)), ALU.add)
            recip(rden.all(), den.all())
            tt("dve", osb((4 * g, 4 * g + 4), (0, 64)), po((0, 4), (0, 64)),
               rden((0, 4), bcast=(1, 64)), ALU.mult)
            if g == NKV - 1:
                pT = newps().view(BF16, 0, (KC, 128))
                for kc in range(KC):
                    tr(pT(kc, (0, 128)), OSBF[n % 2]((kc * 128, kc * 128 + 128)), IDB.all())
                act(OT((0, KC), (n * 128, n * 128 + 128)), pT.all(), AF.Copy)

        prev_item = None
        for idx in range(NBLK * NKV):
            n, g = idx // NKV, idx % NKV
            pt = PT[idx % 3]
            s_stage(n, g, pt)
            if prev_item is not None:
                p_stage(*prev_item)
            prev_item = (n, g, pt)
        p_stage(*prev_item)
        if hh == 0:
            cp("dve", KHALO(j, (0, NKV), (0, 128)), Kt((0, NKV), (UT, UT + 128)))
            cp("dve", VHALO(j, (0, NKV), (0, 65)), Vv(NBLK, (0, NKV), (0, 65)))
        for b in range(2):
            wb = next_block("wo").view(BF16, 0, (KC, 512))
            for t in range(NT):
                tsl = (t * 512, t * 512 + 512)
                for ml in range(4):
                    m = b * 4 + ml
                    ps = newps().view(F32, 0, (512,))
                    for kc in range(KC):
                        mm(ps.all(), wb(kc, (ml * 128, ml * 128 + 128)), OT(kc, tsl), kc == 0, kc == KC - 1)
                    stt("dve", X(m, tsl), ps.all(), vcol("bo", j * 8 + m), X(m, tsl), ALU.add, ALU.add)

    def store_unit(u):
        for b in range(NBLK):
            io = IOB[b % 2].view(F32, 0, (D,))
            for half in range(2):
                ps = newps().view(F32, 0, (4, 128))
                for q4 in range(4):
                    kc = half * 4 + q4
                    tr(ps(q4, (0, 128)), X(kc, (b * 128, b * 128 + 128)), IDF.all())
                dst = io((half * 512, half * 512 + 512))
                if half == 0:
                    act(dst, ps.all(), AF.Copy)
                else:
                    cp("dve", dst, ps.all())
            t = io.all()
            dma("sp", "io%d" % (b % 2), o_d.ap()[u * UT + b * 128: u * UT + (b + 1) * 128, :], t.ap,
                reads=[t.toks])

    for u in range(NU):
        hh = u % 2
        load_unit(u)
        for i in range(L):
            rmsnorm("nmix", i, H)
            if i % 2 == 0:
                conv_mixer(i, hh)
            else:
                attention(i, hh)
            rmsnorm("nffn", i, H)
            ffn(i, hh)
        rmsnorm("nfin", 0, X)
        store_unit(u)

    S.emit(nc, es, ["io0", "io1"])
    es.close()
    return nc


_CACHE = {}


def run(inp, NU, L, n_cores, trace=False):
    key = (NU, L)
    if key not in _CACHE:
        _CACHE[key] = build(NU, L)
    nc = _CACHE[key]
    x = np.ascontiguousarray(np.asarray(inp["x"], dtype=np.float32))
    xs = x.reshape(n_cores, NU * UT, D)
    inp = {k: np.asarray(v, dtype=np.float32) for k, v in inp.items()}
    ws = host_wstream(L, inp)
    vecs = host_vecs(inp)
    consts = host_consts()
    in_maps = [{"x": xs[c], "wstream": ws, "vecs": vecs, "consts": consts} for c in range(n_cores)]
    res = run_bass_kernel_spmd(nc, in_maps, core_ids=list(range(n_cores)), trace=trace)
    out = np.stack([r["out"] for r in res.results], axis=0)
    return out, res


def kernel(**inputs):
    x = inputs["x"]
    B, S_, D_ = x.shape
    out, _ = run(inputs, NU=(B // N_CORES) * (S_ // UT), L=4, n_cores=N_CORES)
    return out.reshape(B, S_, D_).astype(np.float32)
```

```python
import numpy as np
from contextlib import ExitStack
import concourse.bass as bass
import concourse.mybir as mybir
from concourse.bass_utils import run_bass_kernel_spmd

F32 = mybir.dt.float32
BF16 = mybir.dt.bfloat16
AF = mybir.ActivationFunctionType
ALU = mybir.AluOpType
ESZ = {F32: 4, BF16: 2}

D = 1024
KC = 8
DFF = 2816
FC = 22
UT = 1024
NT = 2
NBLK = 8
NHEAD = 16
NKV = 4
EPS = 1e-5
N_CORES = 8
SEQ = 2048
SLOT_E = 5632
NSLOT = 3
SEM_ROLL = 8000
ATTACH_WAITS = True

C_IDENT = 0
C_PROT = 128
C_ONES = 256
C_MCUR = 384
C_MPREV = 896
C_COS = 1408
C_SIN = 1408 + 2048
C_TOTAL = 1408 + 4096


class Op:
    __slots__ = ("eng", "fn", "dma", "deps", "needed", "sig", "dcount", "pos", "clock", "waits", "tag")

    def __init__(self, eng, fn, dma):
        self.eng = eng
        self.fn = fn
        self.dma = dma
        self.deps = []
        self.needed = False
        self.sig = None
        self.dcount = 0
        self.pos = 0
        self.clock = None
        self.waits = None


class Sched:
    ENGS = ("pe", "act", "dve", "pool", "sp")

    def __init__(self):
        self.q = {e: [] for e in self.ENGS}
        self.order = []
        self.lastw = {}
        self.readers = {}
        self.dma_cnt = {}
        self.tag = ""

    def add(self, eng, fn, reads=(), writes=(), dma=None):
        op = Op(eng, fn, dma)
        op.tag = self.tag
        deps = {}
        rt = set()
        for r in reads:
            rt.update(r)
        wt = set()
        for w in writes:
            wt.update(w)
        lastw = self.lastw
        readers = self.readers
        for t in rt:
            o = lastw.get(t)
            if o is not None:
                deps[id(o)] = o
        for t in wt:
            o = lastw.get(t)
            if o is not None and (o.dma or dma or o.eng != eng):
                deps[id(o)] = o
            rd = readers.get(t)
            if rd:
                for o2 in rd.values():
                    if o2.dma or dma or o2.eng != eng:
                        deps[id(o2)] = o2
        op.deps = list(deps.values())
        key = ("dma", dma) if dma else eng
        for t in rt:
            rd = readers.get(t)
            if rd is None:
                readers[t] = {key: op}
            else:
                rd[key] = op
        for t in wt:
            lastw[t] = op
            readers[t] = {}
        if dma:
            c = self.dma_cnt.get(dma, 0) + 1
            self.dma_cnt[dma] = c
            op.dcount = c
            op.pos = c
        else:
            op.pos = len(self.q[eng]) + 1
        self.q[eng].append(op)
        self.order.append(op)
        return op

    def emit(self, nc, es, final_dma_sems):
        known = {e: {} for e in self.ENGS}
        nwait = 0
        for op in self.order:
            kn = known[op.eng]
            waits = []
            for o in sorted(op.deps, key=lambda d: -d.pos):
                key = ("dma", o.dma) if o.dma else o.eng
                if kn.get(key, 0) >= o.pos:
                    continue
                waits.append(o)
                o.needed = True
                for k2, v2 in o.clock.items():
                    if kn.get(k2, 0) < v2:
                        kn[k2] = v2
            op.waits = waits
            nwait += len(waits)
            ck = dict(kn)
            ck[("dma", op.dma) if op.dma else op.eng] = op.pos
            op.clock = ck
        self.nwait = nwait
        nsig = {}
        for e in self.ENGS:
            c = 0
            for op in self.q[e]:
                if op.dma:
                    continue
                if op.needed:
                    op.sig = c
                    c += 1
            nsig[e] = c
        sems = {}
        for e in self.ENGS:
            n = (nsig[e] + SEM_ROLL - 1) // SEM_ROLL
            for i in range(n):
                sems[(e, i)] = es.enter_context(nc.semaphore("s_%s_%d" % (e, i)))
        dsems = {}
        for name in self.dma_cnt:
            dsems[name] = es.enter_context(nc.semaphore("d_" + name))

        def semval(o):
            if o.dma:
                return dsems[o.dma], 16 * o.dcount
            return sems[(o.eng, o.sig // SEM_ROLL)], o.sig % SEM_ROLL + 1

        def run(e_obj, ename):
            for op in self.q[ename]:
                ws = op.waits
                attach = None
                if ATTACH_WAITS and ws and not op.dma:
                    attach = ws[-1]
                    ws = ws[:-1]
                for o in ws:
                    sm, val = semval(o)
                    e_obj.wait_ge(sm, val)
                ins = op.fn(e_obj)
                if attach is not None:
                    sm, val = semval(attach)
                    ins._wait_ge(sm, val)
                if op.dma:
                    ins.then_inc(dsems[op.dma], 16)
                elif op.sig is not None:
                    ins.then_inc(sems[(ename, op.sig // SEM_ROLL)], 1)
            if ename == "sp":
                for name in final_dma_sems:
                    if name in self.dma_cnt:
                        e_obj.wait_ge(dsems[name], 16 * self.dma_cnt[name])

        with nc.Block() as block:
            @block.tensor
            def _(e):
                run(e, "pe")

            @block.scalar
            def _(e):
                run(e, "act")

            @block.vector
            def _(e):
                run(e, "dve")

            @block.gpsimd
            def _(e):
                run(e, "pool")

            @block.sync
            def _(e):
                run(e, "sp")


class Acc:
    __slots__ = ("ap", "toks")

    def __init__(self, ap, toks):
        self.ap = ap
        self.toks = toks


class Buf:
    def __init__(self, nc, es, name, nbytes, gran, psum=False):
        self.name = name
        self.nbytes = nbytes
        self.gran = gran
        if psum:
            h = es.enter_context(nc.psum_tensor(name, [128, nbytes // 4], F32))
            self.h = {F32: h, BF16: h.bitcast(BF16)}
        else:
            h = es.enter_context(nc.sbuf_tensor(name, [128, nbytes // 2], BF16))
            self.h = {BF16: h, F32: h.bitcast(F32)}

    def view(self, dtype, boff, shape):
        return View(self, dtype, boff, tuple(shape))


class View:
    def __init__(self, buf, dtype, boff, shape):
        self.buf = buf
        self.dtype = dtype
        self.es = ESZ[dtype]
        assert boff % self.es == 0
        self.eoff = boff // self.es
        self.shape = shape
        st = []
        s = 1
        for n in reversed(shape):
            st.append(s)
            s *= n
        self.strides = tuple(reversed(st))
        self.size = s
        assert boff + s * self.es <= buf.nbytes, (buf.name, boff, s, self.es, buf.nbytes)
        self.rowlen = buf.nbytes // self.es

    def __call__(self, *idx, p=(0, 128), bcast=None):
        assert len(idx) == len(self.shape)
        off = self.eoff
        dims = []
        for ix, st, n in zip(idx, self.strides, self.shape):
            if isinstance(ix, tuple):
                lo, hi = ix
                assert 0 <= lo < hi <= n, (ix, n)
                off += lo * st
                dims.append([st, hi - lo])
            else:
                assert 0 <= ix < n, (ix, n)
                off += ix * st
        es = self.es
        gran = self.buf.gran
        name = self.buf.name
        toks = set()
        if dims and dims[-1][0] == 1:
            outer = dims[:-1]
            run = dims[-1][1]
        else:
            outer = dims
            run = 1
        offs = [off]
        for st, n in outer:
            offs = [o + i * st for o in offs for i in range(n)]
        for o in offs:
            b0 = (o * es) // gran
            b1 = ((o + run) * es - 1) // gran
            for b in range(b0, b1 + 1):
                toks.add((name, b))
        apdims = [list(d) for d in dims]
        if bcast is not None:
            apdims.insert(bcast[0], [0, bcast[1]])
        p0, p1 = p
        ap = bass.AP(self.buf.h[self.dtype], p0 * self.rowlen + off,
                     [[self.rowlen, p1 - p0]] + apdims)
        return Acc(ap, toks)

    def all(self, p=(0, 128)):
        return self(*[(0, n) for n in self.shape], p=p)


def unit_blocks(L):
    bl = []
    for i in range(L):
        if i % 2 == 0:
            for c in range(8):
                bl.append(("cin", i, c, 3072))
            for b in range(2):
                bl.append(("cout", i, b, 4096))
        else:
            for b in range(2):
                bl.append(("q", i, b, 4096))
            bl.append(("k", i, 0, 4096))
            bl.append(("v", i, 0, 2048))
            for b in range(2):
                bl.append(("wo", i, b, 4096))
        for b in range(11):
            bl.append(("fin", i, b, 4096))
        for b in range(4):
            bl.append(("fdn", i, b, 5632))
    return bl


def _kmaj(w):
    K, N = w.shape
    return w.reshape(K // 128, 128, N).transpose(1, 0, 2)


def host_wstream(L, inp):
    parts = []
    for (kind, i, b, ne) in unit_blocks(L):
        j = i // 2
        if kind == "cin":
            w = inp["conv_w_in"][j]
            a = np.stack([_kmaj(w[:, s * 1024 + b * 128: s * 1024 + (b + 1) * 128]) for s in range(3)], axis=2)
        elif kind == "cout":
            a = _kmaj(inp["conv_w_out"][j][:, b * 512:(b + 1) * 512])
        elif kind == "q":
            a = _kmaj(inp["attn_w_qkv"][j][:, b * 512:(b + 1) * 512])
        elif kind == "k":
            wk = _kmaj(inp["attn_w_qkv"][j][:, 1024:1280]).reshape(128, 8, 4, 1, 64)
            a = np.broadcast_to(wk, (128, 8, 4, 2, 64))
        elif kind == "v":
            a = _kmaj(inp["attn_w_qkv"][j][:, 1280:1536])
        elif kind == "wo":
            a = _kmaj(inp["attn_w_o"][j][:, b * 512:(b + 1) * 512])
        elif kind == "fin":
            w = inp["ffn_w_in"][i]
            a = np.stack([_kmaj(w[:, s * DFF + b * 256: s * DFF + (b + 1) * 256]) for s in range(2)], axis=2)
        elif kind == "fdn":
            a = _kmaj(inp["ffn_w_down"][i][:, b * 256:(b + 1) * 256])
        a = np.ascontiguousarray(a, dtype=np.float32).reshape(128, -1)
        assert a.shape[1] == ne, (kind, a.shape, ne)
        parts.append(a.reshape(-1))
    return np.concatenate(parts)


def vec_layout():
    lay = {}
    o = 0
    for name, n in (("nmix", 32), ("nffn", 32), ("nfin", 8), ("cconv", 48), ("fconv", 264),
                    ("bq", 16), ("bk", 8), ("bo", 16), ("bv", 512), ("sink", 32)):
        lay[name] = o
        o += n
    lay["_total"] = o
    return lay


VL = vec_layout()


def host_vecs(inp):
    def fm(v):
        sh = v.shape
        k = sh[-1] // 128
        v = v.reshape(sh[:-1] + (k, 128))
        return np.moveaxis(v, -1, 0)

    cols = []
    cols.append(fm(inp["norm_mix"]).reshape(128, -1))
    cols.append(fm(inp["norm_ffn"]).reshape(128, -1))
    cols.append(fm(inp["norm_final"]).reshape(128, -1))
    cols.append(fm(inp["conv_w_conv"]).reshape(128, -1))
    cols.append(fm(inp["ffn_w_conv"]).reshape(128, -1))
    bqkv = inp["attn_b_qkv"]
    cols.append(fm(bqkv[:, :1024]).reshape(128, -1))
    bk = bqkv[:, 1024:1280].reshape(2, 4, 1, 64)
    bk = np.broadcast_to(bk, (2, 4, 2, 64)).reshape(2, 4, 128)
    cols.append(np.ascontiguousarray(bk.transpose(2, 0, 1)).reshape(128, -1))
    cols.append(fm(inp["attn_b_o"]).reshape(128, -1))
    bv = bqkv[:, 1280:1536]
    cols.append(np.broadcast_to(bv.reshape(1, 512), (128, 512)))
    cols.append(np.broadcast_to(inp["attn_sinks"].reshape(1, 32), (128, 32)))
    out = np.ascontiguousarray(np.concatenate(cols, axis=1), dtype=np.float32)
    assert out.shape[1] == VL["_total"], out.shape
    return out


def host_consts():
    c = np.zeros((128, C_TOTAL), np.float32)
    c[:, C_IDENT:C_IDENT + 128] = np.eye(128, dtype=np.float32)
    prot = np.zeros((128, 128), np.float32)
    for m in range(128):
        if m % 64 < 32:
            prot[m + 32, m] = -1.0
        else:
            prot[m - 32, m] = 1.0
    c[:, C_PROT:C_PROT + 128] = prot
    c[:, C_ONES:C_ONES + 128] = 1.0 / 1024.0
    key = np.arange(128)[:, None]
    qi = np.arange(128)[None, :]
    NEG = np.float32(-30000.0)
    mcur = np.where(key <= qi, np.float32(0.0), NEG).astype(np.float32)
    mprev = np.where(key > qi, np.float32(0.0), NEG).astype(np.float32)
    c[:, C_MCUR:C_MCUR + 512] = np.concatenate([mcur, mcur, mprev, mprev], axis=1)
    c[:, C_MPREV:C_MPREV + 512] = 0.0
    pos = np.arange(SEQ, dtype=np.float32)
    inv_freq = (1.0 / (np.float32(10000.0) ** (np.arange(0, 64, 2, dtype=np.float32) / np.float32(64)))).astype(np.float32)
    ang = pos[None, :] * inv_freq[:, None]
    fidx = np.arange(128) % 32
    c[:, C_COS:C_COS + SEQ] = np.cos(ang)[fidx]
    c[:, C_SIN:C_SIN + SEQ] = np.sin(ang)[fidx]
    return c


def build(NU, L):
    nc = bass.Bass("TRN2", target_bir_lowering=False)
    S = Sched()
    es = ExitStack()
    blocks = unit_blocks(L)
    boffs = []
    o = 0
    for b in blocks:
        boffs.append(o)
        o += 128 * b[3]
    wtotal = o

    x_d = nc.dram_tensor("x", [NU * UT, D], F32, kind="ExternalInput")
    w_d = nc.dram_tensor("wstream", [wtotal], F32, kind="ExternalInput")
    v_d = nc.dram_tensor("vecs", [128, VL["_total"]], F32, kind="ExternalInput")
    c_d = nc.dram_tensor("consts", [128, C_TOTAL], F32, kind="ExternalInput")
    o_d = nc.dram_tensor("out", [NU * UT, D], F32, kind="ExternalOutput")

    XB = Buf(nc, es, "X", 32768, 2048)
    HB = Buf(nc, es, "H", 16384, 1024)
    BIG = Buf(nc, es, "BIG", 45056, 1024)
    VB = Buf(nc, es, "V", 4736, 520)
    CSB = Buf(nc, es, "CS", 8192, 8192)
    WB = [Buf(nc, es, "W%d" % i, SLOT_E * 2, SLOT_E * 2) for i in range(NSLOT)]
    IOB = [Buf(nc, es, "IO%d" % i, 4096, 4096) for i in range(2)]
    NRM = Buf(nc, es, "NRM", 12288, 1024)
    SCR = Buf(nc, es, "SCR", 24576, 512)
    CNB = Buf(nc, es, "CN", 512 + 768 + 2048, 4096)
    VEC = Buf(nc, es, "VEC", VL["_total"] * 4, 65536)
    HAL = Buf(nc, es, "HAL", 4096, 64)
    PS = [Buf(nc, es, "PS%d" % i, 2048, 2048, psum=True) for i in range(8)]

    X = XB.view(F32, 0, (KC, UT))
    H = HB.view(BF16, 0, (KC, UT))
    A = BIG.view(BF16, 0, (FC, UT))
    Y = BIG.view(BF16, 0, (KC, UT))
    Q = BIG.view(BF16, 0, (KC, UT))
    Kt = BIG.view(BF16, 16384, (NKV, 128 + UT))
    OT = BIG.view(BF16, 13 * 2048, (KC, UT))
    Vv = VB.view(BF16, 0, (NBLK + 1, NKV, 65))
    COS = CSB.view(BF16, 0, (SEQ,))
    SIN = CSB.view(BF16, 4096, (SEQ,))
    SQ = NRM.view(BF16, 0, (KC, 512))
    RSTD = [NRM.view(F32, 8192 + 2048 * i, (512,)) for i in range(2)]
    IDF = CNB.view(F32, 0, (128,))
    IDB = CNB.view(BF16, 512, (128,))
    PROT = CNB.view(BF16, 768, (128,))
    ONES = CNB.view(BF16, 1024, (128,))
    MCUR = CNB.view(BF16, 1280, (512,))
    MPREV = CNB.view(BF16, 2304, (512,))
    MASKB = CNB.view(BF16, 1280, (512,))
    VECS = VEC.view(F32, 0, (VL["_total"],))
    GHALO = HAL.view(F32, 0, (4, FC, 2))
    CVHALO = HAL.view(F32, 704, (2, KC, 2))
    KHALO = HAL.view(BF16, 832, (2, NKV, 128))
    VHALO = HAL.view(BF16, 2880, (2, NKV, 65))
    ESINK = HAL.view(F32, 3920, (2, 16))
    ESINKF = HAL.view(F32, 3920, (32,))
    EPSC = HAL.view(F32, 4048, (1,))

    def vcol(name, idx):
        c = VL[name] + idx
        return VECS((c, c + 1))

    psn = [0]

    def newps():
        b = PS[psn[0] % 8]
        psn[0] += 1
        return b

    def mm(out, lhsT, rhs, start, stop):
        S.add("pe", lambda e: e.matmul(out.ap, lhsT.ap, rhs.ap, start=start, stop=stop),
              reads=[lhsT.toks, rhs.toks], writes=[out.toks])

    def tr(out, in_, ident):
        S.add("pe", lambda e: e.transpose(out.ap, in_.ap, ident.ap),
              reads=[in_.toks, ident.toks], writes=[out.toks])

    def act(out, in_, func, bias=None, scale=1.0):
        rd = [in_.toks]
        if isinstance(scale, Acc):
            rd.append(scale.toks)
            scale = scale.ap
        if bias is not None:
            rd.append(bias.toks)
            S.add("act", lambda e: e.activation(out.ap, in_.ap, func, bias=bias.ap, scale=scale),
                  reads=rd, writes=[out.toks])
        else:
            S.add("act", lambda e: e.activation(out.ap, in_.ap, func, scale=scale),
                  reads=rd, writes=[out.toks])

    def tt(eng, out, in0, in1, op):
        S.add(eng, lambda e: e.tensor_tensor(out.ap, in0.ap, in1.ap, op),
              reads=[in0.toks, in1.toks], writes=[out.toks])

    def ts(eng, out, in0, s1, s2, op0, op1=None):
        rd = [in0.toks]
        a1 = s1
        a2 = s2
        if isinstance(s1, Acc):
            rd.append(s1.toks)
            a1 = s1.ap
        if isinstance(s2, Acc):
            rd.append(s2.toks)
            a2 = s2.ap
        if op1 is None:
            S.add(eng, lambda e: e.tensor_scalar(out.ap, in0.ap, a1, None, op0),
                  reads=rd, writes=[out.toks])
        else:
            S.add(eng, lambda e: e.tensor_scalar(out.ap, in0.ap, a1, a2, op0, op1),
                  reads=rd, writes=[out.toks])

    def stt(eng, out, in0, sc, in1, op0, op1):
        rd = [in0.toks, in1.toks]
        a = sc
        if isinstance(sc, Acc):
            rd.append(sc.toks)
            a = sc.ap
        S.add(eng, lambda e: e.scalar_tensor_tensor(out.ap, in0.ap, a, in1.ap, op0, op1),
              reads=rd, writes=[out.toks])

    def cp(eng, out, in_):
        S.add(eng, lambda e: e.tensor_copy(out.ap, in_.ap), reads=[in_.toks], writes=[out.toks])

    def memset(eng, out, val):
        S.add(eng, lambda e: e.memset(out.ap, val), writes=[out.toks])

    def recip(out, in_):
        S.add("dve", lambda e: e.reciprocal(out.ap, in_.ap), reads=[in_.toks], writes=[out.toks])

    def dma(eng, sem, out_ap, in_ap, reads=(), writes=()):
        S.add(eng, lambda e: e.dma_start(out_ap, in_ap), reads=reads, writes=writes, dma=sem)

    nblk_unit = len(blocks)
    total_blocks = nblk_unit * NU
    wst = {"k": 0, "issued": 0}

    def issue_block(j):
        kind, i, b, ne = blocks[j % nblk_unit]
        off = boffs[j % nblk_unit]
        slot = WB[j % NSLOT]
        dst = slot.view(BF16, 0, (ne,)).all()
        src = bass.AP(w_d, off, [[ne, 128], [1, ne]])
        dma("pool", "w%d" % (j % NSLOT), dst.ap, src, writes=[dst.toks])

    def next_block(kind):
        k = wst["k"]
        wst["k"] += 1
        assert blocks[k % nblk_unit][0] == kind, (blocks[k % nblk_unit], kind)
        while wst["issued"] < min(total_blocks, k + NSLOT):
            issue_block(wst["issued"])
            wst["issued"] += 1
        return WB[k % NSLOT]

    cap = c_d.ap()
    vap = v_d.ap()
    t = IDF.all()
    dma("sp", "cst0", t.ap, cap[:, C_IDENT:C_IDENT + 128], writes=[t.toks])
    t = VECS.all()
    dma("sp", "cst1", t.ap, vap, writes=[t.toks])
    for ci, (view, c0, n) in enumerate(((IDB, C_IDENT, 128), (PROT, C_PROT, 128), (ONES, C_ONES, 128),
                        (MCUR, C_MCUR, 512), (MPREV, C_MPREV, 512), (COS, C_COS, SEQ), (SIN, C_SIN, SEQ))):
        t = view.all()
        dma("pool", "cstp%d" % ci, t.ap, cap[:, c0:c0 + n], writes=[t.toks])
    t = Vv((0, NBLK + 1), (0, NKV), (64, 65))
    memset("dve", t, 1.0)
    memset("dve", EPSC.all(), EPS)
    SNK = VECS((VL["sink"], VL["sink"] + 32))
    act(ESINKF.all(), SNK, AF.Exp)

    def load_unit(u):
        for b in range(NBLK):
            io = IOB[b % 2].view(F32, 0, (D,))
            t = io.all()
            dma("sp", "io%d" % (b % 2), t.ap, x_d.ap()[u * UT + b * 128: u * UT + (b + 1) * 128, :],
                writes=[t.toks])
            for half in range(2):
                ps = newps().view(F32, 0, (4, 128))
                for q4 in range(4):
                    kc = half * 4 + q4
                    tr(ps(q4, (0, 128)), io((kc * 128, kc * 128 + 128)), IDF.all())
                dst = X((half * 4, half * 4 + 4), (b * 128, b * 128 + 128))
                if half == 0:
                    act(dst, ps.all(), AF.Copy)
                else:
                    cp("dve", dst, ps.all())

    def rmsnorm(gname, gidx, dst_view):
        for t in range(NT):
            tsl = (t * 512, t * 512 + 512)
            act(SQ.all(), X((0, KC), tsl), AF.Square)
            ps = newps().view(F32, 0, (512,))
            for kc in range(KC):
                mm(ps.all(), ONES.all(), SQ(kc, (0, 512)), kc == 0, kc == KC - 1)
            r = RSTD[t % 2]
            act(r.all(), ps.all(), AF.Sqrt, bias=EPSC.all())
            recip(r.all(), r.all())
            for kc in range(KC):
                stt("dve", dst_view(kc, tsl), X(kc, tsl), vcol(gname, gidx * 8 + kc), r.all(),
                    ALU.mult, ALU.mult)

    def conv_taps(eng, src, base, wname, widx, stride, u1, u2, u3):
        w0 = vcol(wname, widx)
        w1 = vcol(wname, widx + stride)
        w2 = vcol(wname, widx + 2 * stride)
        S.add("act", lambda e: e.activation(u1.ap, src((base - 2, base + 510)).ap, AF.Identity, scale=w0.ap),
              reads=[src((base - 2, base + 510)).toks, w0.toks], writes=[u1.toks])
        stt(eng, u2, src((base - 1, base + 511)), w1, u1, ALU.mult, ALU.add)
        stt(eng, u3, src((base, base + 512)), w2, u2, ALU.mult, ALU.add)

    def conv_mixer(i, hh):
        j = i // 2
        CV = [SCR.view(F32, 4112 * k, (1026,)) for k in range(2)]
        SBC = [SCR.view(F32, 8224 + 2048 * k, (512,)) for k in range(2)]
        U1 = [SCR.view(F32, 12320 + 2048 * k, (512,)) for k in range(2)]
        U2 = SCR.view(F32, 16416, (512,))
        U3 = SCR.view(F32, 18464, (512,))
        its = [(c, t) for c in range(KC) for t in range(NT)]
        st = {}

        def stage_a(k):
            c, t = its[k]
            if t == 0:
                st["wb"] = next_block("cin").view(BF16, 0, (KC, 3, 128))
                cv = CV[c % 2]
                if hh == 0:
                    memset("dve", cv((0, 2)), 0.0)
                else:
                    cp("dve", cv((0, 2)), CVHALO(j, c, (0, 2)))
            wb = st["wb"]
            tsl = (t * 512, t * 512 + 512)
            pss = [PS[k % 3].view(F32, 0, (512,)), PS[3 + k % 2].view(F32, 0, (512,)),
                   PS[5 + k % 3].view(F32, 0, (512,))]
            for s_ in (1, 2, 0):
                for kc in range(KC):
                    mm(pss[s_].all(), wb(kc, s_, (0, 128)), H(kc, tsl), kc == 0, kc == KC - 1)
            act(SBC[k % 2].all(), pss[1].all(), AF.Copy)
            st[k] = pss

        def stage_b(k):
            c, t = its[k]
            cv = CV[c % 2]
            base = 2 + t * 512
            tt("dve", cv((base, base + 512)), SBC[k % 2].all(), st[k][2].all(), ALU.mult)
            act(U1[k % 2].all(), cv((base - 2, base + 510)), AF.Identity, scale=vcol("cconv", j * 24 + c))
            if t == NT - 1:
                cp("dve", CVHALO(j, c, (0, 2)), cv((1024, 1026)))

        def stage_c(k):
            c, t = its[k]
            cv = CV[c % 2]
            base = 2 + t * 512
            tsl = (t * 512, t * 512 + 512)
            w1 = vcol("cconv", j * 24 + c + 8)
            w2 = vcol("cconv", j * 24 + c + 16)
            stt("dve", U2.all(), cv((base - 1, base + 511)), w1, U1[k % 2].all(), ALU.mult, ALU.add)
            stt("dve", U3.all(), cv((base, base + 512)), w2, U2.all(), ALU.mult, ALU.add)
            tt("dve", Y(c, tsl), U3.all(), st.pop(k)[0].all(), ALU.mult)

        nit = len(its)
        for k in range(nit + 2):
            if k < nit:
                stage_a(k)
            if 0 <= k - 1 < nit:
                stage_b(k - 1)
            if 0 <= k - 2 < nit:
                stage_c(k - 2)
        for b in range(2):
            wb = next_block("cout").view(BF16, 0, (KC, 512))
            for t in range(NT):
                tsl = (t * 512, t * 512 + 512)
                for ml in range(4):
                    m = b * 4 + ml
                    ps = newps().view(F32, 0, (512,))
                    for kc in range(KC):
                        mm(ps.all(), wb(kc, (ml * 128, ml * 128 + 128)), Y(kc, tsl), kc == 0, kc == KC - 1)
                    tt("dve", X(m, tsl), X(m, tsl), ps.all(), ALU.add)

    def ffn(i, hh):
        GB = [SCR.view(F32, 4112 * k, (1026,)) for k in range(2)]
        U1 = [SCR.view(F32, 8224 + 2048 * k, (512,)) for k in range(2)]
        U2 = SCR.view(F32, 12320, (512,))
        U3 = [SCR.view(F32, 14368 + 2048 * k, (512,)) for k in range(2)]
        SS = [SCR.view(F32, 18464 + 2048 * k, (512,)) for k in range(2)]
        its = [(b, t, fl) for b in range(11) for t in range(NT) for fl in range(2)]
        st = {}

        def stage_a(k):
            b, t, fl = its[k]
            f = b * 2 + fl
            if t == 0 and fl == 0:
                st["wb"] = next_block("fin").view(BF16, 0, (KC, 2, 256))
                for fl2 in range(2):
                    f2 = b * 2 + fl2
                    if hh == 0:
                        memset("dve", GB[f2 % 2]((0, 2)), 0.0)
                    else:
                        cp("dve", GB[f2 % 2]((0, 2)), GHALO(i, f2, (0, 2)))
            wb = st["wb"]
            gb = GB[f % 2]
            tsl = (t * 512, t * 512 + 512)
            pg = newps().view(F32, 0, (512,))
            pu = newps().view(F32, 0, (512,))
            for s_, ps in ((0, pg), (1, pu)):
                for kc in range(KC):
                    mm(ps.all(), wb(kc, s_, (fl * 128, fl * 128 + 128)), H(kc, tsl), kc == 0, kc == KC - 1)
            base = 2 + t * 512
            act(gb((base, base + 512)), pg.all(), AF.Copy)
            w0 = vcol("fconv", i * 66 + f)
            act(U1[k % 2].all(), gb((base - 2, base + 510)), AF.Identity, scale=w0)
            if t == NT - 1:
                cp("dve", GHALO(i, f, (0, 2)), gb((1024, 1026)))
            st[k] = pu

        def stage_b(k):
            b, t, fl = its[k]
            f = b * 2 + fl
            gb = GB[f % 2]
            base = 2 + t * 512
            w1 = vcol("fconv", i * 66 + f + 22)
            w2 = vcol("fconv", i * 66 + f + 44)
            stt("dve", U2.all(), gb((base - 1, base + 511)), w1, U1[k % 2].all(), ALU.mult, ALU.add)
            stt("dve", U3[k % 2].all(), gb((base, base + 512)), w2, U2.all(), ALU.mult, ALU.add)
            act(SS[k % 2].all(), U3[k % 2].all(), AF.Silu)

        def stage_c(k):
            b, t, fl = its[k]
            f = b * 2 + fl
            tsl = (t * 512, t * 512 + 512)
            tt("dve", A(f, tsl), SS[k % 2].all(), st.pop(k).all(), ALU.mult)

        nit = len(its)
        for k in range(nit + 2):
            if k < nit:
                stage_a(k)
            if 0 <= k - 1 < nit:
                stage_b(k - 1)
            if 0 <= k - 2 < nit:
                stage_c(k - 2)
        for b in range(4):
            wb = next_block("fdn").view(BF16, 0, (FC, 256))
            for t in range(NT):
                tsl = (t * 512, t * 512 + 512)
                for ml in range(2):
                    m = b * 2 + ml
                    ps = newps().view(F32, 0, (512,))
                    for f in range(FC):
                        mm(ps.all(), wb(f, (ml * 128, ml * 128 + 128)), A(f, tsl), f == 0, f == FC - 1)
                    tt("dve", X(m, tsl), X(m, tsl), ps.all(), ALU.add)

    def rope_evac(ps, bias, dst, pos0, k):
        QSB = [SCR.view(BF16, 1024 * n_, (512,)) for n_ in range(2)]
        T1 = [SCR.view(F32, 2048 + 2048 * n_, (512,)) for n_ in range(2)]
        T2 = [SCR.view(F32, 6144 + 2048 * n_, (512,)) for n_ in range(2)]
        qsb = QSB[k % 2]
        t1 = T1[k % 2]
        t2 = T2[k % 2]
        act(qsb.all(), ps.all(), AF.Identity, bias=bias)
        pr = newps().view(F32, 0, (512,))
        mm(pr.all(), PROT.all(), qsb.all(), True, True)
        tt("pool", t1.all(), qsb.all(), COS((pos0, pos0 + 512)), ALU.mult)
        tt("dve", t2.all(), pr.all(), SIN((pos0, pos0 + 512)), ALU.mult)
        tt("dve", dst, t1.all(), t2.all(), ALU.add)

    def attention(i, hh):
        j = i // 2
        PT = [SCR.view(BF16, 10240 + 2048 * n_, (2, 2, 2, 128)) for n_ in range(3)]
        OSB = [SCR.view(BF16, 16384 + 2048 * n_, (NHEAD, 64)) for n_ in range(2)]
        OSBF = [SCR.view(BF16, 16384 + 2048 * n_, (NHEAD * 64,)) for n_ in range(2)]
        DEN = [SCR.view(F32, 20480 + 32 * n_, (4,)) for n_ in range(2)]
        RDEN = [SCR.view(F32, 20544 + 32 * n_, (4,)) for n_ in range(2)]
        rk = 0
        if hh == 1:
            cp("dve", Kt((0, NKV), (0, 128)), KHALO(j, (0, NKV), (0, 128)))
            cp("dve", Vv(0, (0, NKV), (0, 65)), VHALO(j, (0, NKV), (0, 65)))
        for b in range(2):
            wb = next_block("q").view(BF16, 0, (KC, 512))
            for ml in range(4):
                m = b * 4 + ml
                for t in range(NT):
                    tsl = (t * 512, t * 512 + 512)
                    ps = newps().view(F32, 0, (512,))
                    for kc in range(KC):
                        mm(ps.all(), wb(kc, (ml * 128, ml * 128 + 128)), H(kc, tsl), kc == 0, kc == KC - 1)
                    rope_evac(ps, vcol("bq", j * 8 + m), Q(m, tsl), hh * UT + t * 512, rk)
                    rk += 1
        wb = next_block("k").view(BF16, 0, (KC, NKV, 128))
        for g in range(NKV):
            for t in range(NT):
                tsl = (t * 512, t * 512 + 512)
                ps = newps().view(F32, 0, (512,))
                for kc in range(KC):
                    mm(ps.all(), wb(kc, g, (0, 128)), H(kc, tsl), kc == 0, kc == KC - 1)
                rope_evac(ps, vcol("bk", j * 4 + g), Kt(g, (128 + t * 512, 128 + t * 512 + 512)),
                          hh * UT + t * 512, rk)
                rk += 1
        wb = next_block("v").view(BF16, 0, (KC, 256))
        BV = VECS.buf.view(F32, (VL["bv"] + j * 256) * 4, (NKV, 64))
        for bp in range(NBLK // 2):
            psb = newps()
            ps = psb.view(F32, 0, (2, NKV, 64))
            psf = psb.view(F32, 0, (2, 256))
            for bl in range(2):
                blk = bp * 2 + bl
                for kc in range(KC):
                    mm(psf(bl, (0, 256)), H(kc, (blk * 128, blk * 128 + 128)), wb(kc, (0, 256)),
                       kc == 0, kc == KC - 1)
            for bl in range(2):
                blk = bp * 2 + bl
                tt("dve", Vv(1 + blk, (0, NKV), (0, 64)), ps(bl, (0, NKV), (0, 64)), BV.all(), ALU.add)
        def s_stage(n, g, pt):
            nb = hh * NBLK + n
            nkb = 2 if nb > 0 else 1
            bks = [newps() for _ in range(2)]
            banks = [b_.view(F32, 0, (2, 2, 128)) for b_ in bks]
            flat = [b_.view(F32, 0, (512,)) for b_ in bks]
            for hf in range(2):
                mm(flat[hf]((0, nkb * 256)), IDB.all(), MASKB((0, nkb * 256)), True, False)
            for kb in range(nkb):
                kcol = (128 + n * 128) if kb == 0 else n * 128
                for hl in range(4):
                    h = 4 * g + hl
                    ch = h // 2
                    hf = h % 2
                    prt = (hf * 64, hf * 64 + 64)
                    mm(banks[hf](kb, hl // 2, (0, 128)), Kt(g, (kcol, kcol + 128), p=prt),
                       Q(ch, (n * 128, n * 128 + 128), p=prt), False, (kb == nkb - 1 and hl >= 2))
            for hf in range(2):
                sl = ((0, nkb), (0, 2), (0, 128))
                act(pt(hf, *sl), banks[hf](*sl), AF.Exp, scale=0.125)

        def p_stage(n, g, pt):
            nb = hh * NBLK + n
            nkb = 2 if nb > 0 else 1
            osb = OSB[n % 2]
            po = newps().view(F32, 0, (4, 65))
            for hl in range(4):
                for kb in range(nkb):
                    vblk = (1 + n) if kb == 0 else n
                    mm(po(hl, (0, 65)), pt(hl % 2, kb, hl // 2, (0, 128)), Vv(vblk, g, (0, 65)),
                       kb == 0, kb == nkb - 1)
            den = DEN[g % 2]
            rden = RDEN[g % 2]
            tt("dve", den.all(), po((0, 4), 64), ESINK(j, (4 * g, 4 * g + 4)), ALU.add)
            recip(rden.all(), den.all())
            tt("dve", osb((4 * g, 4 * g + 4), (0, 64)), po((0, 4), (0, 64)),
               rden((0, 4), bcast=(1, 64)), ALU.mult)
            if g == NKV - 1:
                pT = newps().view(BF16, 0, (KC, 128))
                for kc in range(KC):
                    tr(pT(kc, (0, 128)), OSBF[n % 2]((kc * 128, kc * 128 + 128)), IDB.all())
                act(OT((0, KC), (n * 128, n * 128 + 128)), pT.all(), AF.Copy)

        prev_item = None
        for idx in range(NBLK * NKV):
            n, g = idx // NKV, idx % NKV
            pt = PT[idx % 3]
            s_stage(n, g, pt)
            if prev_item is not None:
                p_stage(*prev_item)
            prev_item = (n, g, pt)
        p_stage(*prev_item)
        if hh == 0:
            cp("dve", KHALO(j, (0, NKV), (0, 128)), Kt((0, NKV), (UT, UT + 128)))
            cp("dve", VHALO(j, (0, NKV), (0, 65)), Vv(NBLK, (0, NKV), (0, 65)))
        for b in range(2):
            wb = next_block("wo").view(BF16, 0, (KC, 512))
            for t in range(NT):
                tsl = (t * 512, t * 512 + 512)
                for ml in range(4):
                    m = b * 4 + ml
                    ps = newps().view(F32, 0, (512,))
                    for kc in range(KC):
                        mm(ps.all(), wb(kc, (ml * 128, ml * 128 + 128)), OT(kc, tsl), kc == 0, kc == KC - 1)
                    stt("dve", X(m, tsl), ps.all(), vcol("bo", j * 8 + m), X(m, tsl), ALU.add, ALU.add)

    def store_unit(u):
        for b in range(NBLK):
            io = IOB[b % 2].view(F32, 0, (D,))
            for half in range(2):
                ps = newps().view(F32, 0, (4, 128))
                for q4 in range(4):
                    kc = half * 4 + q4
                    tr(ps(q4, (0, 128)), X(kc, (b * 128, b * 128 + 128)), IDF.all())
                dst = io((half * 512, half * 512 + 512))
                if half == 0:
                    act(dst, ps.all(), AF.Copy)
                else:
                    cp("dve", dst, ps.all())
            t = io.all()
            dma("sp", "io%d" % (b % 2), o_d.ap()[u * UT + b * 128: u * UT + (b + 1) * 128, :], t.ap,
                reads=[t.toks])

    for u in range(NU):
        hh = u % 2
        S.tag = "load"
        load_unit(u)
        for i in range(L):
            S.tag = "norm"
            rmsnorm("nmix", i, H)
            if i % 2 == 0:
                S.tag = "conv"
                conv_mixer(i, hh)
            else:
                S.tag = "attn"
                attention(i, hh)
            S.tag = "norm"
            rmsnorm("nffn", i, H)
            S.tag = "ffn"
            ffn(i, hh)
        S.tag = "norm"
        rmsnorm("nfin", 0, X)
        S.tag = "store"
        store_unit(u)

    S.emit(nc, es, ["io0", "io1"])
    es.close()
    return nc


_CACHE = {}


def run(inp, NU, L, n_cores, trace=False):
    key = (NU, L)
    if key not in _CACHE:
        _CACHE[key] = build(NU, L)
    nc = _CACHE[key]
    x = np.ascontiguousarray(np.asarray(inp["x"], dtype=np.float32))
    xs = x.reshape(n_cores, NU * UT, D)
    inp = {k: np.asarray(v, dtype=np.float32) for k, v in inp.items()}
    ws = host_wstream(L, inp)
    vecs = host_vecs(inp)
    consts = host_consts()
    in_maps = [{"x": xs[c], "wstream": ws, "vecs": vecs, "consts": consts} for c in range(n_cores)]
    res = run_bass_kernel_spmd(nc, in_maps, core_ids=list(range(n_cores)), trace=trace)
    out = np.stack([r["out"] for r in res.results], axis=0)
    return out, res


def kernel(**inputs):
    x = inputs["x"]
    B, S_, D_ = x.shape
    out, _ = run(inputs, NU=(B // N_CORES) * (S_ // UT), L=4, n_cores=N_CORES)
    return out.reshape(B, S_, D_).astype(np.float32)
```
